# Optimizing a Trainium2 kernel written in Bass

```python
import jax, jax.numpy as jnp
from jax import lax
import numpy as np

D_MODEL = 1024
BATCH = 8
SEQ = 2048
DEPTH = 2
DEC_BATCH = 8
DEC_SEQ = 8192
PAST_LEN = 128

D_FF = 2816
EPS = 1e-6
FNET_GROUPS = 4
FNET_GROUP_DIM = D_MODEL // 8
FNET_WIDTH = FNET_GROUPS * FNET_GROUP_DIM
CONV_CH = D_MODEL // 2
CONV_WIDTH = 31
EVEN_IN = FNET_WIDTH + 2 * CONV_CH
EVEN_MIX = FNET_WIDTH + CONV_CH
MLA_HEADS = 8
QK_NOPE = 64
QK_ROPE = 32
QK_HEAD = QK_NOPE + QK_ROPE
V_HEAD = 64
Q_LORA = 256
KV_LORA = 128
MLA_WIDTH = MLA_HEADS * V_HEAD
SC_CH = D_MODEL // 2
SC_WIDTH = 3
ODD_OFF_KV = Q_LORA
ODD_OFF_KR = Q_LORA + KV_LORA
ODD_OFF_SB = ODD_OFF_KR + QK_ROPE
ODD_OFF_SC = ODD_OFF_SB + SC_CH
ODD_OFF_SX = ODD_OFF_SC + SC_CH
ODD_IN = ODD_OFF_SX + SC_CH
ODD_MIX = MLA_WIDTH + SC_CH
ROPE_BASE = 10000.0
Q_BLOCK = 128
N_EVEN = (DEPTH + 1) // 2
N_ODD = DEPTH // 2

kernel_name = "hybrid_fnet_conformer_mla_shortconv_encoder"


def rmsnorm(x, g):
    x32 = x.astype(jnp.float32)
    y = x32 * lax.rsqrt(jnp.mean(x32 * x32, axis=-1, keepdims=True) + EPS)
    return (y * g.astype(jnp.float32)).astype(x.dtype)


def layernorm(x, g, b):
    x32 = x.astype(jnp.float32)
    mu = jnp.mean(x32, axis=-1, keepdims=True)
    xc = x32 - mu
    y = xc * lax.rsqrt(jnp.mean(xc * xc, axis=-1, keepdims=True) + EPS)
    return (y * g.astype(jnp.float32) + b.astype(jnp.float32)).astype(x.dtype)


def swiglu(h, w_gate, w_up, w_down):
    a = jnp.einsum('bsd,df->bsf', h, w_gate)
    u = jnp.einsum('bsd,df->bsf', h, w_up)
    return jnp.einsum('bsf,fd->bsd', jax.nn.silu(a) * u, w_down)


def depthwise_conv(x, w):
    k = w.shape[0]
    pad = (k - 1) // 2
    return lax.conv_general_dilated(
        x, w[:, None, :].astype(x.dtype), window_strides=(1,), padding=[(pad, pad)],
        dimension_numbers=('NWC', 'WIO', 'NWC'), feature_group_count=x.shape[-1])


def rope_tables(s):
    pos = jnp.arange(s, dtype=jnp.float32)
    inv_freq = ROPE_BASE ** (-jnp.arange(0, QK_ROPE, 2, dtype=jnp.float32) / QK_ROPE)
    ang = pos[:, None] * inv_freq[None, :]
    ang = jnp.concatenate([ang, ang], axis=-1)
    return jnp.cos(ang)[:, None, :], jnp.sin(ang)[:, None, :]


def apply_rope_tail(x, cos, sin):
    x_nope, x_rope = x[..., :QK_NOPE], x[..., QK_NOPE:]
    xr = x_rope.astype(jnp.float32)
    x1, x2 = xr[..., :QK_ROPE // 2], xr[..., QK_ROPE // 2:]
    rot = jnp.concatenate([-x2, x1], axis=-1)
    xr = xr * cos + rot * sin
    return jnp.concatenate([x_nope, xr.astype(x.dtype)], axis=-1)


def block_attention(q, k, v):
    b, s, h, dq = q.shape
    nb = s // Q_BLOCK
    scale = dq ** -0.5
    qb = q.reshape(b, nb, Q_BLOCK, h, dq).transpose(1, 0, 2, 3, 4)

    def one_block(qblk):
        sc = jnp.einsum('bqhd,bkhd->bhqk', qblk, k, preferred_element_type=jnp.float32) * scale
        p = jax.nn.softmax(sc, axis=-1)
        return jnp.einsum('bhqk,bkhd->bqhd', p.astype(v.dtype), v)

    out = lax.map(one_block, qb)
    return out.transpose(1, 0, 2, 3, 4).reshape(b, s, h * v.shape[-1])


def even_mixer(h, w_in, conv_w, conv_b, ln_g, ln_b, w_out):
    b, s, _ = h.shape
    u = jnp.einsum('bsd,de->bse', h, w_in)
    u_f, u_v, u_g = jnp.split(u, [FNET_WIDTH, FNET_WIDTH + CONV_CH], axis=-1)
    uf = u_f.reshape(b, s, FNET_GROUPS, FNET_GROUP_DIM).astype(jnp.float32)
    y_a = jnp.fft.fft2(uf, axes=(1, 3), norm='ortho').real
    y_a = y_a.reshape(b, s, FNET_WIDTH).astype(h.dtype)
    g = u_v * jax.nn.sigmoid(u_g)
    g = depthwise_conv(g, conv_w) + conv_b
    y_b = jax.nn.silu(layernorm(g, ln_g, ln_b))
    return jnp.einsum('bse,ed->bsd', jnp.concatenate([y_a, y_b], axis=-1), w_out)


def odd_mixer(h, w_in, q_norm, w_q_up, kv_norm, w_kv_up, q_head_norm, k_head_norm, sc_conv_w, w_out):
    b, s, _ = h.shape
    u = jnp.einsum('bsd,de->bse', h, w_in)
    c_q, c_kv, k_rope, sc_b, sc_c, sc_x = jnp.split(
        u, [ODD_OFF_KV, ODD_OFF_KR, ODD_OFF_SB, ODD_OFF_SC, ODD_OFF_SX], axis=-1)
    q = jnp.einsum('bsr,re->bse', rmsnorm(c_q, q_norm), w_q_up).reshape(b, s, MLA_HEADS, QK_HEAD)
    kv = jnp.einsum('bsr,re->bse', rmsnorm(c_kv, kv_norm), w_kv_up).reshape(b, s, MLA_HEADS, QK_NOPE + V_HEAD)
    k_nope, v = kv[..., :QK_NOPE], kv[..., QK_NOPE:]
    k = jnp.concatenate(
        [k_nope, jnp.broadcast_to(k_rope[:, :, None, :], (b, s, MLA_HEADS, QK_ROPE))], axis=-1)
    q = rmsnorm(q, q_head_norm)
    k = rmsnorm(k, k_head_norm)
    cos, sin = rope_tables(s)
    q = apply_rope_tail(q, cos, sin)
    k = apply_rope_tail(k, cos, sin)
    y_c = block_attention(q, k, v)
    y_d = sc_b * depthwise_conv(sc_c * sc_x, sc_conv_w)
    return jnp.einsum('bse,ed->bsd', jnp.concatenate([y_c, y_d], axis=-1), w_out)


def trunk(x, p):
    for layer in range(DEPTH):
        x = x + 0.5 * swiglu(rmsnorm(x, p['ffn1_norm'][layer]), p['ffn1_w_gate'][layer],
                             p['ffn1_w_up'][layer], p['ffn1_w_down'][layer])
        h = rmsnorm(x, p['mix_norm'][layer])
        if layer % 2 == 0:
            i = layer // 2
            x = x + even_mixer(h, p['ev_w_in'][i], p['ev_conv_w'][i], p['ev_conv_b'][i],
                               p['ev_ln_g'][i], p['ev_ln_b'][i], p['ev_w_out'][i])
        else:
            i = layer // 2
            x = x + odd_mixer(h, p['od_w_in'][i], p['od_q_norm'][i], p['od_w_q_up'][i],
                              p['od_kv_norm'][i], p['od_w_kv_up'][i], p['od_q_head_norm'][i],
                              p['od_k_head_norm'][i], p['od_sc_conv_w'][i], p['od_w_out'][i])
        x = x + 0.5 * swiglu(rmsnorm(x, p['ffn2_norm'][layer]), p['ffn2_w_gate'][layer],
                             p['ffn2_w_up'][layer], p['ffn2_w_down'][layer])
    return x


def setup_inputs(seed: int = 0) -> dict:
    key = jax.random.key(seed)
    ks = jax.random.split(key, 32)

    def w(k, shape, fan_in):
        return jax.random.normal(k, shape, jnp.float32) * (fan_in ** -0.5)

    def gain(k, shape):
        return 1.0 + 0.01 * jax.random.normal(k, shape, jnp.float32)

    def bias(k, shape):
        return 0.01 * jax.random.normal(k, shape, jnp.float32)

    return {
        'x_prompt': jax.random.normal(ks[0], (BATCH, SEQ, D_MODEL), jnp.float32),
        'x_sample': jax.random.normal(ks[1], (DEC_BATCH, DEC_SEQ, D_MODEL), jnp.float32),
        'ffn1_norm': gain(ks[2], (DEPTH, D_MODEL)),
        'ffn1_w_gate': w(ks[3], (DEPTH, D_MODEL, D_FF), D_MODEL),
        'ffn1_w_up': w(ks[4], (DEPTH, D_MODEL, D_FF), D_MODEL),
        'ffn1_w_down': w(ks[5], (DEPTH, D_FF, D_MODEL), D_FF),
        'mix_norm': gain(ks[6], (DEPTH, D_MODEL)),
        'ffn2_norm': gain(ks[7], (DEPTH, D_MODEL)),
        'ffn2_w_gate': w(ks[8], (DEPTH, D_MODEL, D_FF), D_MODEL),
        'ffn2_w_up': w(ks[9], (DEPTH, D_MODEL, D_FF), D_MODEL),
        'ffn2_w_down': w(ks[10], (DEPTH, D_FF, D_MODEL), D_FF),
        'ev_w_in': w(ks[11], (N_EVEN, D_MODEL, EVEN_IN), D_MODEL),
        'ev_conv_w': w(ks[12], (N_EVEN, CONV_WIDTH, CONV_CH), CONV_WIDTH),
        'ev_conv_b': bias(ks[13], (N_EVEN, CONV_CH)),
        'ev_ln_g': gain(ks[14], (N_EVEN, CONV_CH)),
        'ev_ln_b': bias(ks[15], (N_EVEN, CONV_CH)),
        'ev_w_out': w(ks[16], (N_EVEN, EVEN_MIX, D_MODEL), EVEN_MIX),
        'od_w_in': w(ks[17], (N_ODD, D_MODEL, ODD_IN), D_MODEL),
        'od_q_norm': gain(ks[18], (N_ODD, Q_LORA)),
        'od_w_q_up': w(ks[19], (N_ODD, Q_LORA, MLA_HEADS * QK_HEAD), Q_LORA),
        'od_kv_norm': gain(ks[20], (N_ODD, KV_LORA)),
        'od_w_kv_up': w(ks[21], (N_ODD, KV_LORA, MLA_HEADS * (QK_NOPE + V_HEAD)), KV_LORA),
        'od_q_head_norm': gain(ks[22], (N_ODD, QK_HEAD)),
        'od_k_head_norm': gain(ks[23], (N_ODD, QK_HEAD)),
        'od_sc_conv_w': w(ks[24], (N_ODD, SC_WIDTH, SC_CH), SC_WIDTH),
        'od_w_out': w(ks[25], (N_ODD, ODD_MIX, D_MODEL), ODD_MIX),
    }


def reference(x_prompt, x_sample, ffn1_norm, ffn1_w_gate, ffn1_w_up, ffn1_w_down, mix_norm,
              ffn2_norm, ffn2_w_gate, ffn2_w_up, ffn2_w_down,
              ev_w_in, ev_conv_w, ev_conv_b, ev_ln_g, ev_ln_b, ev_w_out,
              od_w_in, od_q_norm, od_w_q_up, od_kv_norm, od_w_kv_up,
              od_q_head_norm, od_k_head_norm, od_sc_conv_w, od_w_out):
    p = {
        'ffn1_norm': ffn1_norm, 'ffn1_w_gate': ffn1_w_gate, 'ffn1_w_up': ffn1_w_up,
        'ffn1_w_down': ffn1_w_down, 'mix_norm': mix_norm,
        'ffn2_norm': ffn2_norm, 'ffn2_w_gate': ffn2_w_gate, 'ffn2_w_up': ffn2_w_up,
        'ffn2_w_down': ffn2_w_down,
        'ev_w_in': ev_w_in, 'ev_conv_w': ev_conv_w, 'ev_conv_b': ev_conv_b,
        'ev_ln_g': ev_ln_g, 'ev_ln_b': ev_ln_b, 'ev_w_out': ev_w_out,
        'od_w_in': od_w_in, 'od_q_norm': od_q_norm, 'od_w_q_up': od_w_q_up,
        'od_kv_norm': od_kv_norm, 'od_w_kv_up': od_w_kv_up,
        'od_q_head_norm': od_q_head_norm, 'od_k_head_norm': od_k_head_norm,
        'od_sc_conv_w': od_sc_conv_w, 'od_w_out': od_w_out,
    }
    y_prompt = trunk(x_prompt, p)
    y_sample = trunk(x_sample, p)
    return (y_prompt, y_sample)
```

```python
import numpy as np
from contextlib import ExitStack
import concourse.bass as bass
import concourse.mybir as mybir
from concourse.bass_utils import run_bass_kernel_spmd

F32 = mybir.dt.float32
BF16 = mybir.dt.bfloat16
AF = mybir.ActivationFunctionType
ALU = mybir.AluOpType
AX = mybir.AxisListType

NDMASEM = 16
D = 1024
DFF = 2816
NF = 22
EPS = 1e-6


class Sched:
    COMPUTE = ("pe", "act", "dve", "pool")
    QUEUES = ("sp", "poolq")
    STREAM = {"pe": "pe", "act": "act", "dve": "dve", "pool": "pool", "sp": "sp", "poolq": "pool"}

    def __init__(self):
        self.ops = []
        self.buf = {}
        self.dma_hist = {q: [] for q in self.QUEUES}
        self.last = {}
        self.capture = None

    def record(self, f):
        self.capture = []
        f()
        out, self.capture = self.capture, None
        return out

    def replay(self, item):
        (self.dma if item[0] else self.op)(item[1], item[2], item[3], item[4])

    def interleave(self, a, b):
        na, nb = len(a), len(b)
        ia = ib = 0
        while ia < na or ib < nb:
            if ib >= nb or (ia < na and ia * nb <= ib * na):
                self.replay(a[ia])
                ia += 1
            else:
                self.replay(b[ib])
                ib += 1

    def _track(self, idx, reads, writes, eng=None):
        deps = set()
        for b in reads:
            st = self.buf.setdefault(b, [None, []])
            if st[0] is not None:
                deps.add(st[0])
            if isinstance(b, tuple) and b[0] == "ps":
                deps.update(r for r in st[1] if self.STREAM[self.ops[r][0]] != self.STREAM.get(eng))
        for b in writes:
            st = self.buf.setdefault(b, [None, []])
            if st[0] is not None:
                deps.add(st[0])
            deps.update(st[1])
        for b in reads:
            self.buf[b][1].append(idx)
        for b in writes:
            self.buf[b] = [idx, []]
        deps.discard(idx)
        return deps

    def op(self, eng, fn, reads=(), writes=()):
        if self.capture is not None:
            self.capture.append((0, eng, fn, tuple(reads), tuple(writes)))
            return
        idx = len(self.ops)
        deps = self._track(idx, reads, writes, eng)
        if eng == "pe":
            deps = {d for d in deps if self.ops[d][0] != "pe"}
        self.ops.append([eng, fn, deps, False, False])
        self.last[self.STREAM[eng]] = idx
        return idx

    def dma(self, q, fn, reads=(), writes=()):
        if self.capture is not None:
            self.capture.append((1, q, fn, tuple(reads), tuple(writes)))
            return
        idx = len(self.ops)
        deps = self._track(idx, reads, writes, q)
        h = self.dma_hist[q]
        if len(h) >= NDMASEM:
            deps.add(h[-NDMASEM])
        h.append(idx)
        self.ops.append([q, fn, deps, True, True])
        return idx

    def barrier(self):
        deps = set()
        for s, i in self.last.items():
            deps.add(i)
        for q in self.QUEUES:
            deps.update(self.dma_hist[q][-NDMASEM:])
        for e in ("pe", "act", "dve", "pool", "sp"):
            self.ops.append([e, None, set(deps), False, False])
        self.buf = {}

    def emit(self, block, sems, dsems):
        ops = self.ops
        for o in ops:
            for d in o[2]:
                ops[d][4] = True
        ev = {}
        cnt = {e: 0 for e in self.COMPUTE}
        dcnt = {q: [0] * NDMASEM for q in self.QUEUES}
        dn = {q: 0 for q in self.QUEUES}
        for i, o in enumerate(ops):
            if o[3]:
                q = o[0]
                k = dn[q] % NDMASEM
                dn[q] += 1
                dcnt[q][k] += 16
                ev[i] = (("d", q, k), dcnt[q][k])
            elif o[4]:
                cnt[o[0]] += 1
                ev[i] = (("c", o[0]), cnt[o[0]])
        final = {("d", q, k): dcnt[q][k] for q in self.QUEUES for k in range(NDMASEM) if dcnt[q][k]}
        self.counts = (cnt, dcnt)

        def semh(key):
            return sems[key[1]] if key[0] == "c" else dsems[key[1]][key[2]]

        streams = {s: [] for s in ("pe", "act", "dve", "pool", "sp")}
        for i, o in enumerate(ops):
            streams[self.STREAM[o[0]]].append(i)

        def run(stream, eng):
            known = {}
            for i in streams[stream]:
                o = ops[i]
                need = {}
                for d in o[2]:
                    k, v = ev[d]
                    if need.get(k, 0) < v:
                        need[k] = v
                for k, v in need.items():
                    if known.get(k, 0) < v:
                        eng.wait_ge(semh(k), v)
                        known[k] = v
                if o[1] is None:
                    continue
                ins = o[1](eng)
                if i in ev:
                    k, v = ev[i]
                    ins.then_inc(semh(k), 16 if k[0] == "d" else 1)
            if stream == "sp":
                for k, v in final.items():
                    eng.wait_ge(semh(k), v)

        block.tensor(lambda e: run("pe", e))
        block.scalar(lambda e: run("act", e))
        block.vector(lambda e: run("dve", e))
        block.gpsimd(lambda e: run("pool", e))
        block.sync(lambda e: run("sp", e))


WNAMES = [
    ("ffn1_norm", [2, 1024]), ("ffn1_w_gate", [2, 1024, 2816]), ("ffn1_w_up", [2, 1024, 2816]),
    ("ffn1_w_down", [2, 2816, 1024]), ("mix_norm", [2, 1024]), ("ffn2_norm", [2, 1024]),
    ("ffn2_w_gate", [2, 1024, 2816]), ("ffn2_w_up", [2, 1024, 2816]), ("ffn2_w_down", [2, 2816, 1024]),
    ("ev_w_in", [1, 1024, 1536]), ("ev_conv_w", [1, 31, 512]), ("ev_conv_b", [1, 512]),
    ("ev_ln_g", [1, 512]), ("ev_ln_b", [1, 512]), ("ev_w_out", [1, 1024, 1024]),
    ("od_w_in", [1, 1024, 1952]), ("od_q_norm", [1, 256]), ("od_w_q_up", [1, 256, 768]),
    ("od_kv_norm", [1, 128]), ("od_w_kv_up", [1, 128, 1024]), ("od_q_head_norm", [1, 96]),
    ("od_k_head_norm", [1, 96]), ("od_sc_conv_w", [1, 3, 512]), ("od_w_out", [1, 1024, 1024]),
]


def host_consts(seqs):
    c = {}
    n = np.arange(128, dtype=np.float64)
    ang = 2 * np.pi * np.outer(n, n) / 128
    C, Sn = np.cos(ang), np.sin(ang)
    c["dftm"] = np.concatenate([C, Sn, -Sn, -C], axis=1).astype(np.float32)
    for si, Sq in enumerate(seqs):
        n2 = Sq // 128
        k1 = np.arange(128, dtype=np.float64)[:, None]
        s2 = np.arange(n2, dtype=np.float64)[None, :]
        a = 2 * np.pi * k1 * s2 / Sq
        sc = 1.0 / np.sqrt(Sq * 128.0)
        c[f"tw{si}"] = np.concatenate([np.cos(a) * sc, np.sin(a) * sc], axis=1).astype(np.float32)
        m = np.arange(n2, dtype=np.float64)
        a2 = 2 * np.pi * np.outer(m, m) / n2
        c[f"dft2_{si}"] = np.concatenate([np.cos(a2), np.sin(a2)], axis=1).astype(np.float32)
    smax = max(seqs)
    pos = np.arange(smax, dtype=np.float32)
    inv = (np.float32(10000.0) ** (-np.arange(0, 32, 2, dtype=np.float32) / np.float32(32))).astype(np.float32)
    angr = (pos[:, None] * inv[None, :]).astype(np.float32)
    c["rope"] = np.concatenate([np.cos(angr), np.sin(angr)], axis=1).astype(np.float32)
    return c


def build(seqs, upto=99, dbg=False):
    nc = bass.Bass("TRN2", target_bir_lowering=False)
    S = Sched()
    T = sum(seqs)
    NT = T // 512
    soff = [sum(seqs[:i]) for i in range(len(seqs))]
    tiles = []
    for si, Sq in enumerate(seqs):
        for t0 in range(0, Sq, 512):
            tiles.append((si, t0, soff[si] + t0))

    xin = [nc.dram_tensor(f"x{si}", [Sq, D], F32, kind="ExternalInput").ap() for si, Sq in enumerate(seqs)]
    yout = [nc.dram_tensor(f"y{si}", [Sq, D], F32, kind="ExternalOutput").ap() for si, Sq in enumerate(seqs)]
    W = {n: nc.dram_tensor(n, sh, F32, kind="ExternalInput").ap() for n, sh in WNAMES}
    hc = host_consts(seqs)
    CT = {n: nc.dram_tensor(n, list(a.shape), F32, kind="ExternalInput").ap() for n, a in hc.items()}
    sk = "ExternalOutput" if dbg else "Internal"
    X = nc.dram_tensor("X", [T, D], F32, kind=sk).ap()
    AB = nc.dram_tensor("AB", [T, 1024], BF16, kind=sk).ap()
    YA = nc.dram_tensor("YA", [T, 512], BF16, kind=sk).ap()
    OT = nc.dram_tensor("OT", [512, T], BF16, kind=sk).ap()
    BT = nc.dram_tensor("BT", [512, T], F32, kind=sk).ap()
    GLT = [nc.dram_tensor(f"GLT{si}", [512, Sq + 32], BF16, kind=sk).ap() for si, Sq in enumerate(seqs)]
    Z = [nc.dram_tensor(f"Z{si}", [128, Sq // 128, 1024], BF16, kind=sk).ap() for si, Sq in enumerate(seqs)]
    QT = [nc.dram_tensor(f"QT{si}", [8, 96, Sq], BF16, kind=sk).ap() for si, Sq in enumerate(seqs)]
    KT = [nc.dram_tensor(f"KT{si}", [8, 96, Sq], BF16, kind=sk).ap() for si, Sq in enumerate(seqs)]
    VV = [nc.dram_tensor(f"VV{si}", [Sq, 8 * 65], BF16, kind=sk).ap() for si, Sq in enumerate(seqs)]
    PT = [nc.dram_tensor(f"PT{si}", [512, Sq + 2], F32, kind=sk).ap() for si, Sq in enumerate(seqs)]

    uid = [0]

    def SB(es, name, shape, dt):
        uid[0] += 1
        return es.enter_context(nc.sbuf_tensor(f"{name}_u{uid[0]}", shape, dt))

    def MM(out, lhsT, rhs, st, sp, r, w):
        S.op("pe", lambda e: e.matmul(out, lhsT, rhs, start=st, stop=sp), r, w)

    def TR(out, in_, ident, r, w):
        S.op("pe", lambda e: e.transpose(out, in_, ident), r, w)

    def ACT(out, in_, func, r, w, **kw):
        S.op("act", lambda e: e.activation(out=out, in_=in_, func=func, **kw), r, w)

    def V(eng, name, r, w, *a, **kw):
        S.op(eng, lambda e: getattr(e, name)(*a, **kw), r, w)

    def LD(q, out, in_, r=(), w=(), slow=False):
        if slow:
            S.dma(q, lambda e: e.dma_start(out=out, in_=in_, allow_slow_non_contiguous=True), r, w)
        else:
            S.dma(q, lambda e: e.dma_start(out=out, in_=in_), r, w)

    def src_in(i, j):
        si, t0, f0 = tiles[i]
        return xin[si][t0 + j * 128:t0 + (j + 1) * 128, :]

    def src_x(i, j):
        si, t0, f0 = tiles[i]
        return X[f0 + j * 128:f0 + (j + 1) * 128, :]

    def dst_out(i, j):
        si, t0, f0 = tiles[i]
        return yout[si][t0 + j * 128:t0 + (j + 1) * 128, :]

    with ExitStack() as gs:
        PS = gs.enter_context(nc.psum_tensor("PS", [128, 8, 512], F32))
        ps = [PS[:, i, :] for i in range(8)]
        PSB = PS[:].rearrange("p b c -> p (b c)").bitcast(BF16)
        psb = [PSB[:, i * 1024:(i + 1) * 1024] for i in range(8)]
        sems = {e: gs.enter_context(nc.semaphore(f"s_{e}")) for e in Sched.COMPUTE}
        dsems = {q: [gs.enter_context(nc.semaphore(f"d_{q}{k}")) for k in range(NDMASEM)] for q in Sched.QUEUES}
        block = gs.enter_context(nc.Block())

        def pipeline(stages, n, order=None):
            ns = len(stages)
            for t in range(n + ns - 1):
                for s_ in (order if order is not None else reversed(range(ns))):
                    i = t - s_
                    if 0 <= i < n:
                        stages[s_](i)

        def mk_ident(es):
            identf = SB(es, "identf", [128, 128], F32)
            ident = SB(es, "ident", [128, 128], BF16)
            V("pool", "memset", [], ["identf"], identf[:], 0.0)
            S.op("pool", lambda e: e.affine_select(out=identf[:], in_=identf[:], compare_op=ALU.not_equal, fill=1.0,
                                                   base=0, pattern=[[-1, 128]], channel_multiplier=1),
                 ["identf"], ["identf"])
            V("dve", "tensor_copy", ["identf"], ["ident"], out=ident[:], in_=identf[:])
            return ident, identf

        class Front:
            def __init__(self, es, gvec, ident, nh=1, nx=8, nhb=2):
                self.xs = [SB(es, f"xs{k}", [128, D], F32) for k in range(nx)]
                self.nx = nx
                self.hb = [SB(es, f"hb{k}", [128, D], BF16) for k in range(nhb)]
                self.nhb = nhb
                self.junk = SB(es, "junk", [128, D], BF16)
                self.hTs = [SB(es, f"hT{k}", [128, 8, 512], BF16) for k in range(nh)]
                self.nh = nh
                self.gb = SB(es, "gb", [128, D], F32)
                self.ss = SB(es, "ss", [128, 12], F32)
                self.rs = SB(es, "rs", [128, 12], F32)
                self.ident = ident
                LD("sp", self.gb[:], gvec.partition_broadcast(128), [], ["gb"])
                self.hT = self.hTs[0]
                self.hTk = [("hT", 0, j) for j in range(4)]

            def keys(self, i):
                return [("hT", i % self.nh, j) for j in range(4)]

            def buf(self, i):
                return self.hTs[i % self.nh]

            def _par(self, i):
                return i % (self.nx // 4)

            def load(self, i, src):
                par = self._par(i)
                c0 = par * 4
                V("dve", "memset", [], [("ss", par)], self.ss[:, c0:c0 + 4], 0.0)
                for j in range(4):
                    LD("sp", self.xs[c0 + j][:], src(i, j), [], [("xs", c0 + j)])

            def square(self, i, j):
                par = self._par(i)
                sl = par * 4 + j
                ACT(self.junk[:], self.xs[sl][:], AF.Square, [("xs", sl), ("ss", par)], ["junk", ("ss", par)],
                    accum_out=self.ss[:, sl:sl + 1])

            def rstd(self, i):
                par = self._par(i)
                c0 = par * 4
                ss, rs = self.ss, self.rs
                V("dve", "tensor_scalar", [("ss", par)], [("rs", par)], out=rs[:, c0:c0 + 4], in0=ss[:, c0:c0 + 4],
                  scalar1=1.0 / D, scalar2=EPS, op0=ALU.mult, op1=ALU.add)
                ACT(rs[:, c0:c0 + 4], rs[:, c0:c0 + 4], AF.Sqrt, [("rs", par)], [("rs", par)])
                V("dve", "reciprocal", [("rs", par)], [("rs", par)], out=rs[:, c0:c0 + 4], in_=rs[:, c0:c0 + 4])

            def stats(self, i, src):
                self.load(i, src)
                for j in range(4):
                    self.square(i, j)
                self.rstd(i)

            def hprep(self, i, j):
                par = self._par(i)
                sl = par * 4 + j
                hs = j % self.nhb
                V("dve", "scalar_tensor_tensor", [("xs", sl), ("rs", par), "gb"], [("hb", hs)],
                  out=self.hb[hs][:], in0=self.xs[sl][:], scalar=self.rs[:, sl:sl + 1], in1=self.gb[:], op0=ALU.mult, op1=ALU.mult)

            def tr(self, i, j):
                hs = j % self.nhb
                h = self.hb[hs]
                hT = self.buf(i)
                hi = i % self.nh
                pb = psb[j % 2]
                for k in range(8):
                    TR(pb[:, k * 128:(k + 1) * 128], h[:, k * 128:(k + 1) * 128], self.ident[:],
                       [("hb", hs), "ident"], [("ps", j % 2)])
                src_v = pb[:, 0:1024].rearrange("p (k t) -> p k t", k=8)
                if j % 2 == 0:
                    ACT(hT[:, :, j * 128:(j + 1) * 128], src_v, AF.Copy, [("ps", j % 2)], [("hT", hi, j)])
                else:
                    V("dve", "tensor_copy", [("ps", j % 2)], [("hT", hi, j)], out=hT[:, :, j * 128:(j + 1) * 128], in_=src_v)

            def rest(self, i):
                for j in range(4):
                    self.hprep(i, j)
                    self.tr(i, j)

            def run(self, i, src):
                self.stats(i, src)
                self.rest(i)

        def phase_ffn(l, pre, src, dst):
            with ExitStack() as es:
                wg = SB(es, "wg", [128, 8, DFF], BF16)
                wu = SB(es, "wu", [128, 8, DFF], BF16)
                wd = SB(es, "wd", [128, NF, D], BF16)
                gT = SB(es, "gT", [128, NF, 512], BF16)
                sg = SB(es, "sg", [128, 512], F32)
                ident, _ = mk_ident(es)
                fr = Front(es, W[pre + "_norm"][l], ident)
                Wg = W[pre + "_w_gate"][l].rearrange("(k p) f -> p k f", p=128)
                Wu = W[pre + "_w_up"][l].rearrange("(k p) f -> p k f", p=128)
                Wd = W[pre + "_w_down"][l].rearrange("(k p) f -> p k f", p=128)
                blks = [(0, 4), (4, 10), (10, 16), (16, 22)]
                blk_of = {fc: bi for bi, (f0, f1) in enumerate(blks) for fc in range(f0, f1)}
                for bi, (f0, f1) in enumerate(blks):
                    LD("poolq", wg[:, :, f0 * 128:f1 * 128], Wg[:, :, f0 * 128:f1 * 128], [], [("wg", bi)])
                    LD("poolq", wu[:, :, f0 * 128:f1 * 128], Wu[:, :, f0 * 128:f1 * 128], [], [("wu", bi)])
                for bi, (f0, f1) in enumerate(blks):
                    LD("poolq", wd[:, f0:f1, :], Wd[:, f0:f1, :], [], [("wd", bi)])

                def gu(i):
                    for fc in range(NF):
                        if i + 1 < NT:
                            if fc == 3:
                                fr.load(i + 1, src)
                            if 6 <= fc < 10:
                                fr.square(i + 1, fc - 6)
                            if fc == 11:
                                fr.rstd(i + 1)
                            if fc in (18, 19):
                                fr.hprep(i + 1, fc - 18)
                        bg, bu = ps[2 + fc % 2], ps[4 + fc % 2]
                        for k in range(8):
                            MM(bg[:, :], wg[:, k, fc * 128:(fc + 1) * 128], fr.hT[:, k, :], k == 0, k == 7,
                               [("wg", blk_of[fc])] + fr.hTk, [("ps", 2 + fc % 2)])
                        for k in range(8):
                            MM(bu[:, :], wu[:, k, fc * 128:(fc + 1) * 128], fr.hT[:, k, :], k == 0, k == 7,
                               [("wu", blk_of[fc])] + fr.hTk, [("ps", 4 + fc % 2)])
                        ACT(sg[:], bg[:, :], AF.Silu, [("ps", 2 + fc % 2)], ["sg"])
                        V("dve", "tensor_tensor", ["sg", ("ps", 4 + fc % 2)], [("gT", fc)], out=gT[:, fc, :], in0=sg[:],
                          in1=bu[:, :], op=ALU.mult)

                def down(i, j):
                    par = i % 2
                    sl = par * 4 + j
                    for half in range(2):
                        b = 6 + half
                        for fc in range(NF):
                            MM(ps[b][:, :], gT[:, fc, j * 128:(j + 1) * 128], wd[:, fc, half * 512:(half + 1) * 512],
                               fc == 0, fc == NF - 1, [("gT", fc), ("wd", blk_of[fc])], [("ps", b)])
                        xv = fr.xs[sl][:, half * 512:(half + 1) * 512]
                        V("dve", "scalar_tensor_tensor", [("ps", b), ("xs", sl)], [("xs", sl)], out=xv, in0=ps[b][:, :],
                          scalar=0.5, in1=xv, op0=ALU.mult, op1=ALU.add)
                    LD("sp", dst(i, j), fr.xs[sl][:], [("xs", sl)], [])

                fr.run(0, src)
                for i in range(NT):
                    gu(i)
                    if i + 1 < NT:
                        fr.tr(i + 1, 0)
                        fr.hprep(i + 1, 2)
                        fr.tr(i + 1, 1)
                        fr.hprep(i + 1, 3)
                        down(i, 0)
                        fr.tr(i + 1, 2)
                        fr.tr(i + 1, 3)
                    else:
                        down(i, 0)
                    for j in range(1, 4):
                        down(i, j)
                S.barrier()

        def phase_e1():
            with ExitStack() as es:
                win = SB(es, "win", [128, 8, 1536], BF16)
                cs = SB(es, "cs", [128, 256], BF16)
                ufT = [SB(es, f"ufT{k}", [128, 4, 512], BF16) for k in range(2)]
                abt = [SB(es, f"abt{k}", [128, 1024], BF16) for k in range(2)]
                sgl = [SB(es, f"sgl{k}", [128, 512], F32) for k in range(2)]
                glt = [SB(es, f"glt{k}", [128, 512], BF16) for k in range(2)]
                zpad = SB(es, "zpad", [128, 16], BF16)
                ident, _ = mk_ident(es)
                fr = Front(es, W["mix_norm"][0], ident, nh=2, nhb=4, nx=12)
                Wi = W["ev_w_in"][0].rearrange("(k p) f -> p k f", p=128)
                for k in range(8):
                    LD("poolq", win[:, k, :], Wi[:, k, :], [], [("win", k)])
                LD("poolq", cs[:], CT["dftm"][:, 0:256], [], ["cs"])
                V("pool", "memset", [], ["zpad"], zpad[:], 0.0)
                for si, Sq in enumerate(seqs):
                    for cc in range(4):
                        LD("sp", GLT[si][cc * 128:(cc + 1) * 128, 0:16], zpad[:], ["zpad"], [])
                        LD("sp", GLT[si][cc * 128:(cc + 1) * 128, Sq + 16:Sq + 32], zpad[:], ["zpad"], [])
                winr = [("win", k) for k in range(8)]

                def sL(i):
                    fr.load(i, src_x)

                def sA(i):
                    for j in range(4):
                        fr.square(i, j)
                    fr.rstd(i)

                def sB(i):
                    fr.rest(i)

                def s1(i):
                    si, t0, f0 = tiles[i]
                    hT, hk = fr.buf(i), fr.keys(i)
                    uf = ufT[i % 2]
                    for cc in range(4):
                        for k in range(8):
                            MM(ps[6][:, :], win[:, k, 512 + cc * 128:512 + (cc + 1) * 128], hT[:, k, :], k == 0, k == 7,
                               winr + hk, [("ps", 6)])
                        for k in range(8):
                            MM(ps[7][:, :], win[:, k, 1024 + cc * 128:1024 + (cc + 1) * 128], hT[:, k, :], k == 0, k == 7,
                               winr + hk, [("ps", 7)])
                        ACT(sgl[cc % 2][:], ps[7][:, :], AF.Sigmoid, [("ps", 7)], [("sgl", cc % 2)])
                        V("dve", "tensor_tensor", [("sgl", cc % 2), ("ps", 6)], [("glt", cc % 2)], out=glt[cc % 2][:],
                          in0=sgl[cc % 2][:], in1=ps[6][:, :], op=ALU.mult)
                        LD("sp", GLT[si][cc * 128:(cc + 1) * 128, 16 + t0:16 + t0 + 512], glt[cc % 2][:], [("glt", cc % 2)], [])
                        g = cc
                        b = 2 + g % 2
                        for k in range(8):
                            MM(ps[b][:, :], win[:, k, g * 128:(g + 1) * 128], hT[:, k, :], k == 0, k == 7,
                               winr + hk, [("ps", b)])
                        if g % 2 == 0:
                            ACT(uf[:, g, :], ps[b][:, :], AF.Copy, [("ps", b)], [("ufT", i % 2, g)])
                        else:
                            V("dve", "tensor_copy", [("ps", b)], [("ufT", i % 2, g)], out=uf[:, g, :], in_=ps[b][:, :])

                def s2(i):
                    si, t0, f0 = tiles[i]
                    uf = ufT[i % 2]
                    for j in range(4):
                        a = abt[j % 2]
                        for gp in range(2):
                            b = (4 if j % 2 == 0 else 2) + gp
                            for g2 in range(2):
                                g = gp * 2 + g2
                                MM(ps[b][:, g2 * 256:(g2 + 1) * 256], uf[:, g, j * 128:(j + 1) * 128], cs[:, :], True, True,
                                   [("ufT", i % 2, g), "cs"], [("ps", b)])
                            outv = a[:, :].rearrange("p (ab g c) -> p g ab c", ab=2, g=4)[:, gp * 2:gp * 2 + 2]
                            inv = ps[b][:, :].rearrange("p (g ab c) -> p g ab c", g=2, ab=2)
                            if gp == 0:
                                ACT(outv, inv, AF.Copy, [("ps", b)], [("abt", j % 2)])
                            else:
                                V("dve", "tensor_copy", [("ps", b)], [("abt", j % 2)], out=outv, in_=inv)
                        LD("sp", AB[f0 + j * 128:f0 + (j + 1) * 128, :], a[:], [("abt", j % 2)], [])

                pipeline([sL, sA, sB, s1, s2], NT, order=[4, 2, 3, 1, 0])
                S.barrier()

        def phase_e2a():
            with ExitStack() as es:
                dm = SB(es, "dm", [128, 512], BF16)
                NS = 3
                x1 = [SB(es, f"x1_{k}", [128, 4, 1024], BF16) for k in range(NS)]
                zt = [SB(es, f"zt{k}", [128, 4, 1024], BF16) for k in range(NS)]
                t1 = [SB(es, f"t1_{k}", [128, 512], F32) for k in range(2)]
                t2 = [SB(es, f"t2_{k}", [128, 512], F32) for k in range(2)]
                LD("poolq", dm[:], CT["dftm"][:, :], [], ["dm"])
                Cm, nSm, nCm = dm[:, 0:128], dm[:, 256:384], dm[:, 384:512]
                tws, groups = [], []
                for si, Sq in enumerate(seqs):
                    n2 = Sq // 128
                    tw = SB(es, f"tw{si}", [128, 2 * n2], F32)
                    LD("sp", tw[:], CT[f"tw{si}"][:, :], [], [("tw", si)])
                    tws.append(tw)
                    for g4 in range(n2 // 4):
                        groups.append((si, g4))

                def load(gi):
                    si, g4 = groups[gi]
                    n2 = seqs[si] // 128
                    ABs = AB[soff[si]:soff[si] + seqs[si], :].rearrange("(s1 s2) c -> s1 s2 c", s2=n2)
                    LD("sp", x1[gi % NS][:], ABs[:, g4 * 4:(g4 + 1) * 4, :], [], [("x1", gi % NS)])

                for gi in range(min(NS - 1, len(groups))):
                    load(gi)
                for gi, (si, g4) in enumerate(groups):
                    if gi + NS - 1 < len(groups):
                        load(gi + NS - 1)
                    n2 = seqs[si] // 128
                    tw = tws[si]
                    sl = gi % NS
                    for q in range(4):
                        s2 = g4 * 4 + q
                        br, bi = ps[(s2 % 2)], ps[2 + (s2 % 2)]
                        A_, B_ = x1[sl][:, q, 0:512], x1[sl][:, q, 512:1024]
                        MM(br[:, :], Cm, A_, True, False, ["dm", ("x1", sl)], [("ps", s2 % 2)])
                        MM(br[:, :], nSm, B_, False, True, ["dm", ("x1", sl)], [("ps", s2 % 2)])
                        MM(bi[:, :], nSm, A_, True, False, ["dm", ("x1", sl)], [("ps", 2 + s2 % 2)])
                        MM(bi[:, :], nCm, B_, False, True, ["dm", ("x1", sl)], [("ps", 2 + s2 % 2)])
                        tc_, ts_ = tw[:, s2:s2 + 1], tw[:, n2 + s2:n2 + s2 + 1]
                        ta, tb = t1[s2 % 2], t2[s2 % 2]
                        ACT(ta[:], bi[:, :], AF.Copy, [("ps", 2 + s2 % 2), ("tw", si)], [("t1", s2 % 2)], scale=ts_)
                        V("dve", "scalar_tensor_tensor", [("ps", s2 % 2), ("t1", s2 % 2), ("tw", si)], [("zt", sl)],
                          out=zt[sl][:, q, 0:512], in0=br[:, :], scalar=tc_, in1=ta[:], op0=ALU.mult, op1=ALU.add)
                        ACT(tb[:], br[:, :], AF.Copy, [("ps", s2 % 2), ("tw", si)], [("t2", s2 % 2)], scale=ts_)
                        V("dve", "scalar_tensor_tensor", [("ps", 2 + s2 % 2), ("t2", s2 % 2), ("tw", si)], [("zt", sl)],
                          out=zt[sl][:, q, 512:1024], in0=bi[:, :], scalar=tc_, in1=tb[:], op0=ALU.mult, op1=ALU.subtract)
                    LD("sp", Z[si][:, g4 * 4:(g4 + 1) * 4, :], zt[sl][:], [("zt", sl)], [])
                S.barrier()

        def phase_e2b():
            with ExitStack() as es:
                NS = 3
                nmax = max(seqs) // 128
                z2 = [SB(es, f"z2_{k}", [nmax, 8, 1024], BF16) for k in range(NS)]
                ya = [SB(es, f"ya_{k}", [nmax, 8, 512], BF16) for k in range(NS)]
                d2s, groups = [], []
                for si, Sq in enumerate(seqs):
                    n2 = Sq // 128
                    d2 = SB(es, f"d2_{si}", [n2, 2 * n2], BF16)
                    LD("poolq", d2[:], CT[f"dft2_{si}"][:, :], [], [("d2", si)])
                    d2s.append(d2)
                    for g8 in range(16):
                        groups.append((si, g8))

                def load(gi):
                    si, g8 = groups[gi]
                    n2 = seqs[si] // 128
                    Zs = Z[si].rearrange("k1 s2 c -> s2 k1 c")
                    LD("sp", z2[gi % NS][0:n2], Zs[:, g8 * 8:(g8 + 1) * 8, :], [], [("z2", gi % NS)])

                for gi in range(min(NS - 1, len(groups))):
                    load(gi)
                for gi, (si, g8) in enumerate(groups):
                    if gi + NS - 1 < len(groups):
                        load(gi + NS - 1)
                    Sq = seqs[si]
                    n2 = Sq // 128
                    d2 = d2s[si]
                    sl = gi % NS
                    YAs = YA[soff[si]:soff[si] + Sq, :].rearrange("(k2 k1) c -> k2 k1 c", k1=128)
                    for q in range(8):
                        b = q % 4
                        MM(ps[b][0:n2, :], d2[:, 0:n2], z2[sl][0:n2, q, 0:512], True, False, [("d2", si), ("z2", sl)], [("ps", b)])
                        MM(ps[b][0:n2, :], d2[:, n2:2 * n2], z2[sl][0:n2, q, 512:1024], False, True, [("d2", si), ("z2", sl)], [("ps", b)])
                        if q % 2 == 0:
                            ACT(ya[sl][0:n2, q, :], ps[b][0:n2, :], AF.Copy, [("ps", b)], [("ya", sl)])
                        else:
                            V("dve", "tensor_copy", [("ps", b)], [("ya", sl)], out=ya[sl][0:n2, q, :], in_=ps[b][0:n2, :])
                    LD("sp", YAs[:, g8 * 8:(g8 + 1) * 8, :], ya[sl][0:n2], [("ya", sl)], [])
                S.barrier()

        def load_cols(es, name, vec, ncol):
            return None

        def phase_e3():
            with ExitStack() as es:
                wo = SB(es, "wo", [128, 8, 1024], BF16)
                dg = SB(es, "dg", [128, 4, 31, 128], BF16)
                cwr = SB(es, "cwr", [34, 512], F32)
                cw = SB(es, "cw", [128, 4, 34], F32)
                ones = SB(es, "ones", [128, 128], F32)
                xs = [SB(es, f"xs{k}", [128, D], F32) for k in range(8)]
                yat = [SB(es, f"yat{k}", [128, 4, 512], BF16) for k in range(2)]
                glh = [SB(es, f"glh{k}", [128, 4, 544], BF16) for k in range(3)]
                ymT = [SB(es, f"ymT{k}", [128, 8, 512], BF16) for k in range(3)]
                v = [SB(es, f"v{k}", [128, 4, 512], F32) for k in range(3)]
                sq = [SB(es, f"sq{k}", [128, 4, 512], F32) for k in range(2)]
                rb = [SB(es, f"rb{k}", [128, 512], F32) for k in range(2)]
                ident, identf = mk_ident(es)
                Wo = W["ev_w_out"][0].rearrange("(k p) f -> p k f", p=128)
                for k in range(8):
                    LD("poolq", wo[:, k, :], Wo[:, k, :], [], [("wo", k)])
                LD("sp", cwr[0:31, :], W["ev_conv_w"][0], [], ["cwr"])
                LD("sp", cwr[31:32, :], W["ev_conv_b"][0:1, :], [], ["cwr"])
                LD("sp", cwr[32:33, :], W["ev_ln_g"][0:1, :], [], ["cwr"])
                LD("sp", cwr[33:34, :], W["ev_ln_b"][0:1, :], [], ["cwr"])
                V("dve", "memset", [], ["ones"], ones[:], 1.0 / 512)
                for cc in range(4):
                    TR(ps[0][:, cc * 34:(cc + 1) * 34], cwr[:, cc * 128:(cc + 1) * 128], identf[0:34, 0:34], ["cwr", "identf"], [("ps", 0)])
                V("dve", "tensor_copy", [("ps", 0)], ["cw"], out=cw[:].rearrange("p c t -> p (c t)"), in_=ps[0][:, 0:136])
                for cc in range(4):
                    for t in range(31):
                        if t % 2:
                            ACT(dg[:, cc, t, :], identf[:], AF.Copy, ["identf", "cw"], ["dg"], scale=cw[:, cc, t:t + 1])
                        else:
                            V("dve", "tensor_scalar_mul", ["identf", "cw"], ["dg"], out=dg[:, cc, t, :], in0=identf[:], scalar1=cw[:, cc, t:t + 1])
                def sL(i):
                    si, t0, f0 = tiles[i]
                    p2, p3 = i % 2, i % 3
                    LD("sp", yat[p2][:], YA[f0:f0 + 512, :].rearrange("(j p) c -> p j c", p=128), [], [("yat", p2)])
                    for cc in range(4):
                        LD("sp", glh[p3][:, cc, :], GLT[si][cc * 128:(cc + 1) * 128, t0:t0 + 544], [], [("glh", p3, cc)])

                def s0(i):
                    si, t0, f0 = tiles[i]
                    p2, p3 = i % 2, i % 3
                    for g in range(4):
                        pb = psb[g % 2]
                        for j in range(4):
                            TR(pb[:, j * 128:(j + 1) * 128], yat[p2][:, j, g * 128:(g + 1) * 128], ident[:], [("yat", p2), "ident"], [("ps", g % 2)])
                        if g % 2 == 0:
                            ACT(ymT[p3][:, g, :], pb[:, 0:512], AF.Copy, [("ps", g % 2)], [("ymT", p3, g)])
                        else:
                            V("dve", "tensor_copy", [("ps", g % 2)], [("ymT", p3, g)], out=ymT[p3][:, g, :], in_=pb[:, 0:512])
                    for cc in range(4):
                        b = 2 + cc % 2
                        for t in range(31):
                            MM(ps[b][:, :], dg[:, cc, t, :], glh[p3][:, cc, 1 + t:1 + t + 512], t == 0, t == 30, ["dg", ("glh", p3, cc)], [("ps", b)])
                        V("dve", "tensor_scalar_add", [("ps", b), "cw"], [("v", p3, cc)], out=v[p3][:, cc, :], in0=ps[b][:, :], scalar1=cw[:, cc, 31:32])

                def s1(i):
                    p3 = i % 3
                    for cc in range(4):
                        MM(ps[4][:, :], ones[:], v[p3][:, cc, :], cc == 0, cc == 3, ["ones", ("v", p3, cc)], [("ps", 4)])
                    for cc in range(4):
                        V("dve", "tensor_tensor", [("v", p3, cc), ("ps", 4)], [("v", p3, cc)], out=v[p3][:, cc, :], in0=v[p3][:, cc, :], in1=ps[4][:, :], op=ALU.subtract)
                        ACT(sq[i % 2][:, cc, :], v[p3][:, cc, :], AF.Square, [("v", p3, cc)], [("sq", i % 2, cc)])

                def s2(i):
                    p3 = i % 3
                    r_ = rb[i % 2]
                    for cc in range(4):
                        MM(ps[5][:, :], ones[:], sq[i % 2][:, cc, :], cc == 0, cc == 3, ["ones", ("sq", i % 2, cc)], [("ps", 5)])
                    V("dve", "tensor_scalar_add", [("ps", 5)], [("rb", i % 2)], out=r_[:], in0=ps[5][:, :], scalar1=EPS)
                    ACT(r_[:], r_[:], AF.Sqrt, [("rb", i % 2)], [("rb", i % 2)])
                    V("dve", "reciprocal", [("rb", i % 2)], [("rb", i % 2)], out=r_[:], in_=r_[:])
                    for cc in range(4):
                        V("dve", "tensor_tensor", [("v", p3, cc), ("rb", i % 2)], [("v", p3, cc)], out=v[p3][:, cc, :], in0=v[p3][:, cc, :], in1=r_[:], op=ALU.mult)
                        V("dve", "tensor_scalar", [("v", p3, cc), "cw"], [("v", p3, cc)], out=v[p3][:, cc, :], in0=v[p3][:, cc, :], scalar1=cw[:, cc, 32:33],
                          scalar2=cw[:, cc, 33:34], op0=ALU.mult, op1=ALU.add)
                        ACT(ymT[p3][:, 4 + cc, :], v[p3][:, cc, :], AF.Silu, [("v", p3, cc)], [("ymT", p3, 4 + cc)])
                    par = i % 2
                    for j in range(4):
                        LD("sp", xs[par * 4 + j][:], src_x(i, j), [], [("xs", par * 4 + j)])

                def s3(i):
                    p3 = i % 3
                    par = i % 2
                    for j in range(4):
                        sl = par * 4 + j
                        for half in range(2):
                            b = 6 + half
                            for e_ in range(8):
                                MM(ps[b][:, :], ymT[p3][:, e_, j * 128:(j + 1) * 128], wo[:, e_, half * 512:(half + 1) * 512], e_ == 0, e_ == 7,
                                   [("ymT", p3, e_), ("wo", e_)], [("ps", b)])
                            xv = xs[sl][:, half * 512:(half + 1) * 512]
                            V("dve", "tensor_tensor", [("ps", b), ("xs", sl)], [("xs", sl)], out=xv, in0=xv, in1=ps[b][:, :], op=ALU.add)
                        LD("sp", src_x(i, j), xs[sl][:], [("xs", sl)], [])

                pipeline([sL, s0, s1, s2, s3], NT)
                S.barrier()

        def phase_o1():
            with ExitStack() as es:
                win = SB(es, "win", [128, 8, 1952], BF16)
                wq = SB(es, "wq", [128, 2, 768], BF16)
                wkv = SB(es, "wkv", [128, 1024], BF16)
                gq = SB(es, "gq", [128, 384], F32)
                gh = SB(es, "gh", [128, 2, 96], F32)
                rp = [SB(es, f"rp{k}", [128, 4, 32], F32) for k in range(3)]
                cq = SB(es, "cq", [128, 4, 416], F32)
                sq384 = SB(es, "sq384", [128, 4, 384], F32)
                tmp384 = sq384
                cn = SB(es, "cn", [128, 4, 384], BF16)
                cT = SB(es, "cT", [128, 4, 3, 128], BF16)
                s2_ = SB(es, "s2_", [128, 8], F32)
                r2_ = SB(es, "r2_", [128, 8], F32)
                hXp = [[SB(es, f"hX{p_}_{k}", [128, 32, 96], F32) for k in range(2)] for p_ in range(2)]
                sqX1 = SB(es, "sqX", [128, 32, 96], F32)
                sqX = [sqX1, sqX1]
                sqb = sqX1[:].rearrange("p a d -> p (a d)").bitcast(BF16)
                tmX = [sqb[:, k * 3072:(k + 1) * 3072].rearrange("p (a d) -> p a d", d=96) for k in range(2)]
                s8X = [SB(es, f"s8X{k}", [128, 32], F32) for k in range(2)]
                r8X = [SB(es, f"r8X{k}", [128, 32], F32) for k in range(2)]
                raX = [SB(es, f"raX{k}", [128, 32, 16], F32) for k in range(2)]
                rbX = [SB(es, f"rbX{k}", [128, 32, 16], F32) for k in range(2)]
                xT = [SB(es, f"xT{k}", [96, 8, 512], BF16) for k in range(2)]
                vt1 = SB(es, "vt", [128, 4, 8, 65], BF16)
                vt = [vt1, vt1]
                bt1 = SB(es, "bt", [128, 512], F32)
                bt = [bt1, bt1]
                tm1 = SB(es, "tm", [128, 512], F32)
                tm = [tm1, tm1]
                pt = [SB(es, f"pt{k}", [128, 512], F32) for k in range(2)]
                zp = SB(es, "zp", [128, 1], F32)
                ident, _ = mk_ident(es)
                fr = Front(es, W["mix_norm"][1], ident, nx=8)
                Wi = W["od_w_in"][0].rearrange("(k p) f -> p k f", p=128)
                for k in range(8):
                    LD("poolq", win[:, k, :], Wi[:, k, :], [], [("win", k)])
                LD("poolq", wq[:], W["od_w_q_up"][0].rearrange("(k p) f -> p k f", p=128), [], ["wq"])
                LD("poolq", wkv[:], W["od_w_kv_up"][0], [], ["wkv"])
                LD("sp", gq[:, 0:256], W["od_q_norm"][0].partition_broadcast(128), [], ["gq"])
                LD("sp", gq[:, 256:384], W["od_kv_norm"][0].partition_broadcast(128), [], ["gq"])
                LD("sp", gh[:, 0, :], W["od_q_head_norm"][0].partition_broadcast(128), [], ["gh"])
                LD("sp", gh[:, 1, :], W["od_k_head_norm"][0].partition_broadcast(128), [], ["gh"])
                V("dve", "tensor_scalar_mul", ["gh"], ["gh"], out=gh[:, 0, :], in0=gh[:, 0, :], scalar1=float(96 ** -0.5))
                ghT = SB(es, "ghT", [96, 2], F32)
                LD("sp", ghT[:, 0:1], W["od_q_head_norm"][0].rearrange("(d o) -> d o", o=1), [], ["ghT"], slow=True)
                LD("sp", ghT[:, 1:2], W["od_k_head_norm"][0].rearrange("(d o) -> d o", o=1), [], ["ghT"], slow=True)
                V("dve", "tensor_scalar_mul", ["ghT"], ["ghT"], out=ghT[:, 0:1], in0=ghT[:, 0:1], scalar1=float(96 ** -0.5))
                V("dve", "memset", ["ghT"], ["ghT"], ghT[64:96, :], 1.0)
                V("pool", "memset", [], [("vt", 0)], vt1[:], 1.0)
                V("pool", "memset", [], ["zp"], zp[:], 0.0)
                for si, Sq in enumerate(seqs):
                    for cc in range(4):
                        LD("sp", PT[si][cc * 128:(cc + 1) * 128, 0:1], zp[:], ["zp"], [], slow=True)
                        LD("sp", PT[si][cc * 128:(cc + 1) * 128, Sq + 1:Sq + 2], zp[:], ["zp"], [], slow=True)
                winr = [("win", k) for k in range(8)]
                XT = [QT, KT]

                def chain(c, par, si, t0, i3):
                    h_, sq_, n_, s8, r8, ra, rb2, tmx, xt = hXp[par][c], sqX[c], hXp[par][c], s8X[c], r8X[c], raX[c], rbX[c], tmX[c], xT[c]
                    K = lambda n: (n, c, par) if n == "hX" else ((n, c) if n != "tmX" else "sqX")
                    n4 = n_[:].rearrange("p (j h) d -> p j h d", j=4)
                    t4 = tmx[:].rearrange("p (j h) d -> p j h d", j=4)
                    ra4 = ra[:].rearrange("p (j h) d -> p j h d", j=4)
                    rb4 = rb2[:].rearrange("p (j h) d -> p j h d", j=4)
                    cosb = rp[i3][:, :, 0:16].unsqueeze(2).broadcast_to([128, 4, 8, 16])
                    sinb = rp[i3][:, :, 16:32].unsqueeze(2).broadcast_to([128, 4, 8, 16])
                    x1, x2 = n4[:, :, :, 64:80], n4[:, :, :, 80:96]
                    ops = []
                    ops.append(lambda: ACT(sq_[:], h_[:], AF.Square, [K("hX")], ["sqX"]))
                    ops.append(lambda: V("dve", "tensor_reduce", ["sqX"], [K("s8")], out=s8[:], in_=sq_[:], axis=AX.X, op=ALU.add))
                    ops.append(lambda: V("dve", "tensor_scalar", [K("s8")], [K("r8")], out=r8[:], in0=s8[:], scalar1=1.0 / 96, scalar2=EPS, op0=ALU.mult, op1=ALU.add))
                    ops.append(lambda: ACT(r8[:], r8[:], AF.Sqrt, [K("r8")], [K("r8")]))
                    ops.append(lambda: V("dve", "reciprocal", [K("r8")], [K("r8")], out=r8[:], in_=r8[:]))
                    ops.append(lambda: V("dve", "tensor_tensor", [K("hX"), K("r8")], [K("tmX")], out=tmx[:, :, 0:64], in0=h_[:, :, 0:64], in1=r8[:].unsqueeze(2).broadcast_to([128, 32, 64]), op=ALU.mult))
                    ops.append(lambda: V("dve", "tensor_tensor", [K("hX"), K("r8")], [K("hX")], out=n_[:, :, 64:96], in0=h_[:, :, 64:96], in1=r8[:].unsqueeze(2).broadcast_to([128, 32, 32]), op=ALU.mult))
                    ops.append(lambda: V("dve", "tensor_tensor", [K("hX"), "gh"], [K("hX")], out=n_[:, :, 64:96], in0=n_[:, :, 64:96], in1=gh[:, c, 64:96].unsqueeze(1).broadcast_to([128, 32, 32]), op=ALU.mult))
                    ops.append(lambda: V("dve", "tensor_tensor", [K("hX"), ("rp", i3)], [K("ra")], out=ra4, in0=x1, in1=cosb, op=ALU.mult))
                    ops.append(lambda: V("dve", "tensor_tensor", [K("hX"), ("rp", i3)], [K("rb")], out=rb4, in0=x2, in1=sinb, op=ALU.mult))
                    ops.append(lambda: V("dve", "tensor_tensor", [K("ra"), K("rb")], [K("tmX")], out=t4[:, :, :, 64:80], in0=ra4, in1=rb4, op=ALU.subtract))
                    ops.append(lambda: V("dve", "tensor_tensor", [K("hX"), ("rp", i3)], [K("ra")], out=ra4, in0=x2, in1=cosb, op=ALU.mult))
                    ops.append(lambda: V("dve", "tensor_tensor", [K("hX"), ("rp", i3)], [K("rb")], out=rb4, in0=x1, in1=sinb, op=ALU.mult))
                    ops.append(lambda: V("dve", "tensor_tensor", [K("ra"), K("rb")], [K("tmX")], out=t4[:, :, :, 80:96], in0=ra4, in1=rb4, op=ALU.add))

                    def trs(j):
                        bank = 6 + c
                        for h in range(8):
                            TR(psb[bank][0:96, h * 128:(h + 1) * 128], tmx[:, j * 8 + h, :], ident[:], [K("tmX"), "ident"], [("ps", bank)])
                        inv = psb[bank][0:96, 0:1024].rearrange("p (h t) -> p h t", h=8)
                        if (j + c) % 2 == 0:
                            ACT(xt[:, :, j * 128:(j + 1) * 128], inv, AF.Copy, [("ps", bank), "ghT"], [K("xT")], scale=ghT[:, c:c + 1])
                        else:
                            V("dve", "tensor_scalar_mul", [("ps", bank), "ghT"], [K("xT")], out=xt[:, :, j * 128:(j + 1) * 128], in0=inv, scalar1=ghT[:, c:c + 1])
                        if j == 3:
                            LD("sp", XT[c][si][:, :, t0:t0 + 512].rearrange("h d t -> d h t"), xt[:], [K("xT")], [])
                    for j_ in range(4):
                        ops.append(lambda j_=j_: trs(j_))
                    return ops

                def s0(i):
                    si, t0, f0 = tiles[i]
                    par = i % 2
                    hX = hXp[par]
                    for j in range(4):
                        fr.square(i, j)
                    fr.rstd(i)
                    fr.rest(i)
                    for j in range(4):
                        for k in range(8):
                            MM(ps[2 + j][:, 0:416], fr.hT[:, k, j * 128:(j + 1) * 128], win[:, k, 0:416], k == 0, k == 7, winr + fr.hTk, [("ps", 2 + j)])
                    ACT(cq[:], PS[:, 2:6, 0:416], AF.Copy, [("ps", 2), ("ps", 3), ("ps", 4), ("ps", 5)], ["cq"])
                    for cc in range(4):
                        col = 416 + cc * 128
                        b = 4 + cc % 2
                        for k in range(8):
                            MM(ps[b][:, :], win[:, k, col:col + 128], fr.hT[:, k, :], k == 0, k == 7, winr + fr.hTk, [("ps", b)])
                        ACT(bt[cc % 2][:], ps[b][:, :], AF.Copy, [("ps", b)], [("bt", 0)])
                        LD("sp", BT[cc * 128:(cc + 1) * 128, f0:f0 + 512], bt[cc % 2][:], [("bt", 0)], [])
                    ACT(sq384[:], cq[:, :, 0:384], AF.Square, ["cq"], ["sq384"])
                    V("dve", "tensor_reduce", ["sq384"], ["s2_"], out=s2_[:, 0:4], in_=sq384[:, :, 0:256], axis=AX.X, op=ALU.add)
                    V("dve", "tensor_reduce", ["sq384"], ["s2_"], out=s2_[:, 4:8], in_=sq384[:, :, 256:384], axis=AX.X, op=ALU.add)
                    V("dve", "tensor_scalar", ["s2_"], ["r2_"], out=r2_[:, 0:4], in0=s2_[:, 0:4], scalar1=1.0 / 256, scalar2=EPS, op0=ALU.mult, op1=ALU.add)
                    V("dve", "tensor_scalar", ["s2_"], ["r2_"], out=r2_[:, 4:8], in0=s2_[:, 4:8], scalar1=1.0 / 128, scalar2=EPS, op0=ALU.mult, op1=ALU.add)
                    ACT(r2_[:], r2_[:], AF.Sqrt, ["r2_"], ["r2_"])
                    V("dve", "reciprocal", ["r2_"], ["r2_"], out=r2_[:], in_=r2_[:])
                    V("dve", "tensor_tensor", ["cq", "r2_", "sq384"], ["sq384"], out=tmp384[:, :, 0:256], in0=cq[:, :, 0:256],
                      in1=r2_[:, 0:4].unsqueeze(2).broadcast_to([128, 4, 256]), op=ALU.mult)
                    V("dve", "tensor_tensor", ["cq", "r2_", "sq384"], ["sq384"], out=tmp384[:, :, 256:384], in0=cq[:, :, 256:384],
                      in1=r2_[:, 4:8].unsqueeze(2).broadcast_to([128, 4, 128]), op=ALU.mult)
                    V("dve", "tensor_tensor", ["sq384", "gq"], ["cn"], out=cn[:], in0=tmp384[:], in1=gq[:].unsqueeze(1).broadcast_to([128, 4, 384]), op=ALU.mult)
                    for j in range(4):
                        for r in range(3):
                            c0 = ((j % 2) * 3 + r) * 128
                            TR(psb[j // 2][:, c0:c0 + 128], cn[:, j, r * 128:(r + 1) * 128], ident[:], ["cn", "ident"], [("ps", j // 2)])
                    ACT(cT[:, 0:2].rearrange("p j r t -> p (j r t)"), psb[0][:, 0:768], AF.Copy, [("ps", 0)], ["cT"])
                    V("dve", "tensor_copy", [("ps", 1)], ["cT"], out=cT[:, 2:4].rearrange("p j r t -> p (j r t)"), in_=psb[1][:, 0:768])
                    hq4 = hX[0][:].rearrange("p (j g h) d -> p j g h d", j=4, g=2)
                    hk4 = hX[1][:].rearrange("p (j g h) d -> p j g h d", j=4, g=2)
                    for j in range(4):
                        qb = 2
                        kb = 4 if j % 2 == 0 else 0
                        for g in range(2):
                            for r in range(2):
                                MM(ps[qb + g][:, 0:384], cT[:, j, r, :], wq[:, r, g * 384:(g + 1) * 384], r == 0, r == 1, ["cT", "wq"], [("ps", qb + g)])
                        for g in range(2):
                            MM(ps[kb + g][:, :], cT[:, j, 2, :], wkv[:, g * 512:(g + 1) * 512], True, True, ["cT", "wkv"], [("ps", kb + g)])
                        if E_SUB >= 1:
                          ACT(hq4[:, j], PS[:, qb:qb + 2, 0:384].rearrange("p g (h d) -> p g h d", h=4), AF.Copy, [("ps", qb), ("ps", qb + 1)], [("hX", 0, par)])
                        kvv = PS[:, kb:kb + 2, :].rearrange("p g (h e) -> p g h e", h=4)
                        if E_SUB >= 2:
                          for g_ in range(2):
                              ACT(hk4[:, j, g_, :, 0:64], ps[kb + g_][:, :].rearrange("p (h e) -> p h e", h=4)[:, :, 0:64], AF.Copy, [("ps", kb + g_)], [("hX", 1, par)])
                        if E_SUB >= 3:
                          V("dve", "tensor_copy", [("ps", kb), ("ps", kb + 1)], [("vt", 0)], out=vt[par][:, j].rearrange("p (g h) e -> p g h e", g=2)[:, :, :, 0:64], in_=kvv[:, :, :, 64:128])
                    if E_SUB >= 4:
                      V("pool", "tensor_copy", ["cq"], [("hX", 1, par)], out=hX[1][:].rearrange("p (j h) d -> p j h d", j=4)[:, :, :, 64:96],
                        in_=cq[:, :, 384:416].unsqueeze(2).broadcast_to([128, 4, 8, 32]))
                    if E_SUB >= 5:
                      LD("sp", VV[si][t0:t0 + 512, :].rearrange("(j p) e -> p j e", p=128), vt[par][:].rearrange("p j h e -> p j (h e)"), [("vt", 0)], [])
                    for cc in range(4):
                        colc, colx = 928 + cc * 128, 1440 + cc * 128
                        b1, b2 = (2, 3) if cc % 2 == 0 else (4, 5)
                        for k in range(8):
                            MM(ps[b1][:, :], win[:, k, colc:colc + 128], fr.hT[:, k, :], k == 0, k == 7, winr + fr.hTk, [("ps", b1)])
                        for k in range(8):
                            MM(ps[b2][:, :], win[:, k, colx:colx + 128], fr.hT[:, k, :], k == 0, k == 7, winr + fr.hTk, [("ps", b2)])
                        ACT(tm[cc % 2][:], ps[b1][:, :], AF.Copy, [("ps", b1)], [("tm", 0)])
                        V("dve", "tensor_tensor", [("tm", 0), ("ps", b2)], [("pt", cc % 2)], out=pt[cc % 2][:], in0=tm[cc % 2][:], in1=ps[b2][:, :], op=ALU.mult)
                        LD("sp", PT[si][cc * 128:(cc + 1) * 128, 1 + t0:1 + t0 + 512], pt[cc % 2][:], [("pt", cc % 2)], [])

                def s1(i):
                    si, t0, f0 = tiles[i]
                    par = i % 2
                    ca_, cb_ = chain(0, par, si, t0, i % 3), chain(1, par, si, t0, i % 3)
                    for ii in range(len(ca_) + 2):
                        if ii < len(ca_):
                            ca_[ii]()
                        if ii >= 2:
                            cb_[ii - 2]()

                def loads(i):
                    si, t0, f0 = tiles[i]
                    fr.load(i, src_x)
                    LD("sp", rp[i % 3][:], CT["rope"][t0:t0 + 512, :].rearrange("(j p) c -> p j c", p=128), [], [("rp", i % 3)])

                loads(0)
                for t in range(NT + 1):
                    if t + 1 < NT:
                        loads(t + 1)
                    la = S.record(lambda: s0(t)) if t < NT else []
                    lb = S.record(lambda: s1(t - 1)) if t >= 1 else []
                    S.interleave(la, lb)
                S.barrier()

        def phase_o2():
            with ExitStack() as es:
                smax = max(seqs)
                kTb = [SB(es, f"kTb{k}", [128, smax], BF16) for k in range(2)]
                qTb = [SB(es, f"qTb{k}", [128, smax], BF16) for k in range(2)]
                vb = [SB(es, f"vb{k}", [128, smax // 128, 128], BF16) for k in range(2)]
                for k in range(2):
                    V("pool", "memset", [], [("kTb", k)], kTb[k][64:128, :], 0.0)
                    V("pool", "memset", [], [("qTb", k)], qTb[k][64:128, :], 0.0)
                    V("pool", "memset", [], [("vb", k)], vb[k][:], 1.0)
                pT = [SB(es, f"pT{k}", [128, 2, 512], BF16) for k in range(3)]
                ri = [SB(es, f"ri{k}", [64, 512], F32) for k in range(2)]
                ot = [SB(es, f"ot{k}", [64, 512], BF16) for k in range(2)]
                items = []
                heads = []
                hn = 0
                for si, Sq in enumerate(seqs):
                    for h in range(8):
                        sl = hn % 2
                        hn += 1
                        heads.append((si, h, sl))
                        for qt in range(Sq // 512):
                            for p in range(Sq // 256):
                                items.append((si, h, sl, qt, p))
                loaded = set()

                def load_head(hi):
                    if hi >= len(heads) or hi in loaded:
                        return
                    loaded.add(hi)
                    si2, h2, sl2 = heads[hi]
                    S2 = seqs[si2]
                    LD("sp", kTb[sl2][0:96, 0:S2], KT[si2][h2], [], [("kTb", sl2)])
                    LD("sp", qTb[sl2][0:96, 0:S2], QT[si2][h2], [], [("qTb", sl2)])
                    LD("sp", vb[sl2][:, 0:S2 // 128, 0:65], VV[si2][:, h2 * 65:(h2 + 1) * 65].rearrange("(c p) e -> p c e", p=128), [], [("vb", sl2)])

                load_head(0)

                def qk(n):
                    si, h, sl, qt, p = items[n]
                    Sq = seqs[si]
                    nk = Sq // 128
                    q_ = qTb[sl][:, qt * 512:(qt + 1) * 512]
                    b = (n % 3) * 2
                    for u in range(2):
                        kc = 2 * p + u
                        MM(ps[b + u][:, :], kTb[sl][:, kc * 128:(kc + 1) * 128], q_, True, True, [("kTb", sl), ("qTb", sl)], [("ps", b + u)])
                    ACT(pT[n % 3][:], PS[:, b:b + 2, :], AF.Exp, [("ps", b), ("ps", b + 1)], [("pT", n % 3)])

                qn = [0]

                def pv(n):
                    si, h, sl, qt, p = items[n]
                    Sq = seqs[si]
                    nk = Sq // 128
                    bo = 6 + qn[0] % 2
                    if qt == 0 and p == 0:
                        load_head(heads.index((si, h, sl)) + 1)
                    for u in range(2):
                        kc = 2 * p + u
                        MM(ps[bo][:, :], vb[sl][:, kc, :], pT[n % 3][:, u, :], kc == 0, kc == nk - 1, [("vb", sl), ("pT", n % 3)], [("ps", bo)])
                    if p == Sq // 256 - 1:
                        k2 = qn[0] % 2
                        qn[0] += 1
                        V("dve", "reciprocal", [("ps", bo)], [("ri", k2)], out=ri[k2][:], in_=ps[bo][64:128, :])
                        V("dve", "tensor_tensor", [("ps", bo), ("ri", k2)], [("ot", k2)], out=ot[k2][:], in0=ps[bo][0:64, :], in1=ri[k2][:], op=ALU.mult)
                        LD("sp", OT[h * 64:(h + 1) * 64, soff[si] + qt * 512:soff[si] + (qt + 1) * 512], ot[k2][:], [("ot", k2)], [])

                for n in range(len(items) + 2):
                    if n < len(items):
                        qk(n)
                    if n >= 2:
                        pv(n - 2)
                S.barrier()

        def phase_o3():
            with ExitStack() as es:
                wo = SB(es, "wo", [128, 8, 1024], BF16)
                swr = SB(es, "swr", [3, 512], F32)
                sw = SB(es, "sw", [128, 4, 3], F32)
                xs = [SB(es, f"xs{k}", [128, D], F32) for k in range(12)]
                ymT = [SB(es, f"ymT{k}", [128, 8, 512], BF16) for k in range(3)]
                ph = [SB(es, f"ph{k}", [128, 4, 514], F32) for k in range(3)]
                bt = [SB(es, f"bt{k}", [128, 4, 512], F32) for k in range(3)]
                acc = [SB(es, f"acc{k}", [128, 512], F32) for k in range(2)]
                ident, identf = mk_ident(es)
                Wo = W["od_w_out"][0].rearrange("(k p) f -> p k f", p=128)
                for k in range(8):
                    LD("poolq", wo[:, k, :], Wo[:, k, :], [], [("wo", k)])
                LD("sp", swr[:], W["od_sc_conv_w"][0], [], ["swr"])
                for cc in range(4):
                    TR(ps[0][:, cc * 3:(cc + 1) * 3], swr[:, cc * 128:(cc + 1) * 128], identf[0:3, 0:3], ["swr", "identf"], [("ps", 0)])
                V("dve", "tensor_copy", [("ps", 0)], ["sw"], out=sw[:].rearrange("p c t -> p (c t)"), in_=ps[0][:, 0:12])
                def sL(i):
                    si, t0, f0 = tiles[i]
                    par = i % 3
                    for ec in range(4):
                        LD("sp", ph[par][:, ec, :], PT[si][ec * 128:(ec + 1) * 128, t0:t0 + 514], [], [("ph", par, ec)])
                        LD("sp", bt[par][:, ec, :], BT[ec * 128:(ec + 1) * 128, f0:f0 + 512], [], [("bt", par, ec)])
                    for ec in range(4):
                        LD("sp", ymT[par][:, ec, :], OT[ec * 128:(ec + 1) * 128, f0:f0 + 512], [], [("ymT", par, ec)])
                    for j in range(4):
                        LD("sp", xs[par * 4 + j][:], src_x(i, j), [], [("xs", par * 4 + j)])

                def s0(i):
                    si, t0, f0 = tiles[i]
                    par = i % 3
                    for cc in range(4):
                        eng = "dve"
                        a = acc[cc % 2]
                        V(eng, "tensor_scalar_mul", [("ph", par, cc), "sw"], [("acc", cc % 2)], out=a[:], in0=ph[par][:, cc, 0:512], scalar1=sw[:, cc, 0:1])
                        V(eng, "scalar_tensor_tensor", [("ph", par, cc), "sw", ("acc", cc % 2)], [("acc", cc % 2)], out=a[:], in0=ph[par][:, cc, 1:513],
                          scalar=sw[:, cc, 1:2], in1=a[:], op0=ALU.mult, op1=ALU.add)
                        V(eng, "scalar_tensor_tensor", [("ph", par, cc), "sw", ("acc", cc % 2)], [("acc", cc % 2)], out=a[:], in0=ph[par][:, cc, 2:514],
                          scalar=sw[:, cc, 2:3], in1=a[:], op0=ALU.mult, op1=ALU.add)
                        V(eng, "tensor_tensor", [("acc", cc % 2), ("bt", par, cc)], [("ymT", par, 4 + cc)], out=ymT[par][:, 4 + cc, :], in0=a[:], in1=bt[par][:, cc, :], op=ALU.mult)

                def s1(i):
                    si, t0, f0 = tiles[i]
                    par = i % 3
                    for j in range(4):
                        sl = par * 4 + j
                        for half in range(2):
                            b = 6 + half
                            for e_ in range(8):
                                MM(ps[b][:, :], ymT[par][:, e_, j * 128:(j + 1) * 128], wo[:, e_, half * 512:(half + 1) * 512], e_ == 0, e_ == 7,
                                   [("ymT", par, e_), ("wo", e_)], [("ps", b)])
                            xv = xs[sl][:, half * 512:(half + 1) * 512]
                            V("dve", "tensor_tensor", [("ps", b), ("xs", sl)], [("xs", sl)], out=xv, in0=xv, in1=ps[b][:, :], op=ALU.add)
                        LD("sp", src_x(i, j), xs[sl][:], [("xs", sl)], [])

                pipeline([sL, s0, s1], NT)
                S.barrier()

        phases = [
            lambda: phase_ffn(0, "ffn1", src_in, src_x),
            phase_e1, phase_e2a, phase_e2b, phase_e3,
            lambda: phase_ffn(0, "ffn2", src_x, src_x),
            lambda: phase_ffn(1, "ffn1", src_x, src_x),
            phase_o1, phase_o2, phase_o3,
            lambda: phase_ffn(1, "ffn2", src_x, dst_out),
        ]
        for pi, p in enumerate(phases):
            if pi >= upto:
                break
            if ONLY is not None and pi not in ONLY:
                continue
            p()
        S.emit(block, sems, dsems)
    return nc, hc


_CACHE = {}
ONLY = None
O1_STOP = 99
E_SUB = 99


def kernel(**inputs):
    seqs = (2048, 8192)
    if "prog" not in _CACHE:
        _CACHE["prog"] = build(list(seqs))
    nc, hc = _CACHE["prog"]
    xp = np.ascontiguousarray(inputs["x_prompt"], dtype=np.float32)
    xsm = np.ascontiguousarray(inputs["x_sample"], dtype=np.float32)
    shared = {n: np.ascontiguousarray(inputs[n], dtype=np.float32) for n, _ in WNAMES}
    shared.update(hc)
    in_maps = []
    for c in range(8):
        m = dict(shared)
        m["x0"] = xp[c]
        m["x1"] = xsm[c]
        in_maps.append(m)
    res = run_bass_kernel_spmd(nc, in_maps, core_ids=list(range(8)))
    yp = np.stack([np.asarray(r["y0"], dtype=np.float32) for r in res.results], axis=0)
    ys = np.stack([np.asarray(r["y1"], dtype=np.float32) for r in res.results], axis=0)
    return (yp, ys)
```
